# Optimizing a Trainium2 kernel written in Bass

```python
import math
import jax, jax.numpy as jnp
from jax import lax
import numpy as np

D_MODEL = 1024
BATCH = 8
SEQ = 4096
DEPTH = 2

CHUNK = 64
MEM_LEN = 256
N_MIXERS = 2
N_A = (DEPTH + 1) // 2
N_B = DEPTH // 2

D_LRU = D_MODEL // 2
LRU_BLOCKS = 8
LRU_BLK = D_LRU // LRU_BLOCKS
CONV_WIDTH = 4
LRU_C = 8.0

FOX_HEADS = 8
FOX_HD = 64
FOX_WIDTH = FOX_HEADS * FOX_HD
Q_BLOCK = 128

XA_HEADS = 4
XA_HD = 128
XA_WIDTH = XA_HEADS * XA_HD

D_FF = 4 * D_MODEL
EPS = 1e-6
NEG_INF = -1e30

kernel_name = "hybrid_rglru_fox_memxattn_trunk"


def rms_norm(x, g):
    xf = x.astype(jnp.float32)
    y = xf * lax.rsqrt(jnp.mean(xf * xf, axis=-1, keepdims=True) + EPS)
    return (y * g.astype(jnp.float32)).astype(x.dtype)


def _lin_combine(left, right):
    a1, b1 = left
    a2, b2 = right
    return a1 * a2, a2 * b1 + b2


def rglru_group(h, w_in, conv_w, conv_b, w_r, b_r, w_i, b_i, lam):
    B, S, _ = h.shape
    proj = h @ w_in
    u, gate, xq = jnp.split(proj, [D_LRU, 2 * D_LRU], axis=-1)
    u = lax.conv_general_dilated(
        u, conv_w[:, None, :].astype(u.dtype), window_strides=(1,),
        padding=[(CONV_WIDTH - 1, 0)], dimension_numbers=("NWC", "WIO", "NWC"),
        feature_group_count=D_LRU) + conv_b
    ub = u.reshape(B, S, LRU_BLOCKS, LRU_BLK)
    r = jax.nn.sigmoid(jnp.einsum("bsnc,ncd->bsnd", ub, w_r).reshape(B, S, D_LRU) + b_r)
    i = jax.nn.sigmoid(jnp.einsum("bsnc,ncd->bsnd", ub, w_i).reshape(B, S, D_LRU) + b_i)
    log_a = -LRU_C * r.astype(jnp.float32) * jax.nn.softplus(-lam.astype(jnp.float32))
    a = jnp.exp(log_a)
    b = jnp.sqrt(-jnp.expm1(2.0 * log_a)) * (i.astype(jnp.float32) * u.astype(jnp.float32))
    _, hs = lax.associative_scan(_lin_combine, (a, b), axis=1)
    y = hs.astype(h.dtype) * jax.nn.gelu(gate, approximate=True)
    return y, xq


def fox_group(h, w_in, b_f, q_gain, k_gain):
    B, S, _ = h.shape
    n_qb = S // Q_BLOCK
    proj = h @ w_in
    q, k, v, f_logit, xq = jnp.split(
        proj, [FOX_WIDTH, 2 * FOX_WIDTH, 3 * FOX_WIDTH, 3 * FOX_WIDTH + FOX_HEADS], axis=-1)
    q = rms_norm(q.reshape(B, S, FOX_HEADS, FOX_HD), q_gain)
    k = rms_norm(k.reshape(B, S, FOX_HEADS, FOX_HD), k_gain)
    v = v.reshape(B, S, FOX_HEADS, FOX_HD)
    log_f = jax.nn.log_sigmoid(f_logit.astype(jnp.float32) + b_f.astype(jnp.float32))
    c = jnp.cumsum(log_f, axis=1).transpose(0, 2, 1)
    kh = k.transpose(0, 2, 1, 3)
    vh = v.transpose(0, 2, 1, 3)
    q_blocks = q.reshape(B, n_qb, Q_BLOCK, FOX_HEADS, FOX_HD).transpose(1, 0, 3, 2, 4)
    c_blocks = c.reshape(B, FOX_HEADS, n_qb, Q_BLOCK).transpose(2, 0, 1, 3)
    q_pos = jnp.arange(S, dtype=jnp.int32).reshape(n_qb, Q_BLOCK)
    k_pos = jnp.arange(S, dtype=jnp.int32)
    scale = 1.0 / math.sqrt(FOX_HD)

    def one_block(args):
        qb, cb, pos = args
        s = jnp.einsum("bhqd,bhkd->bhqk", qb, kh).astype(jnp.float32) * scale
        s = s + cb[..., :, None] - c[:, :, None, :]
        s = jnp.where(pos[:, None] >= k_pos[None, :], s, NEG_INF)
        p = jax.nn.softmax(s, axis=-1)
        return jnp.einsum("bhqk,bhkd->bhqd", p.astype(vh.dtype), vh)

    o = lax.map(one_block, (q_blocks, c_blocks, q_pos))
    o = o.transpose(1, 0, 3, 2, 4).reshape(B, S, FOX_WIDTH)
    return o, xq


def memory_group(xq, mem, g_mem, w_kv, q_gain, k_gain):
    B, S, _ = xq.shape
    M = mem.shape[1]
    kv = rms_norm(mem, g_mem) @ w_kv
    k, v = jnp.split(kv, [XA_WIDTH], axis=-1)
    q = rms_norm(xq.reshape(B, S, XA_HEADS, XA_HD), q_gain)
    k = rms_norm(k.reshape(B, M, XA_HEADS, XA_HD), k_gain)
    v = v.reshape(B, M, XA_HEADS, XA_HD)
    s = jnp.einsum("bshd,bmhd->bhsm", q, k).astype(jnp.float32) * (1.0 / math.sqrt(XA_HD))
    p = jax.nn.softmax(s, axis=-1)
    o = jnp.einsum("bhsm,bmhd->bshd", p.astype(v.dtype), v)
    return o.reshape(B, S, XA_WIDTH)


def sqrelu_mlp(h, w1, w2):
    return jnp.square(jax.nn.relu(h @ w1)) @ w2


def setup_inputs(seed: int = 0) -> dict:
    key = jax.random.key(seed)
    ks = jax.random.split(key, 32)
    f32 = jnp.float32

    def nrm(k, shape, fan_in):
        return jax.random.normal(k, shape, f32) * (fan_in ** -0.5)

    def gain(k, shape):
        return 1.0 + 0.05 * jax.random.normal(k, shape, f32)

    def small(k, shape):
        return 0.02 * jax.random.normal(k, shape, f32)

    x = jax.random.normal(ks[0], (BATCH, SEQ, D_MODEL), f32)
    mem = jax.random.normal(ks[1], (BATCH, MEM_LEN, D_MODEL), f32)
    a_c = jax.random.uniform(ks[2], (N_A, D_LRU), f32, 0.9, 0.999)
    s_lam = a_c ** (1.0 / LRU_C)
    lru_lambda = jnp.log(s_lam) - jnp.log1p(-s_lam)
    return {
        "x": x,
        "mem": mem,
        "norm_mix": gain(ks[3], (DEPTH, D_MODEL)),
        "norm_mem": gain(ks[4], (DEPTH, D_MODEL)),
        "w_mem_kv": nrm(ks[5], (DEPTH, D_MODEL, 2 * XA_WIDTH), D_MODEL),
        "xa_q_gain": gain(ks[6], (DEPTH, XA_HD)),
        "xa_k_gain": gain(ks[7], (DEPTH, XA_HD)),
        "w_out": nrm(ks[8], (DEPTH, D_LRU + XA_WIDTH, D_MODEL), D_LRU + XA_WIDTH),
        "norm_mlp": gain(ks[9], (DEPTH, D_MODEL)),
        "w_mlp_in": nrm(ks[10], (DEPTH, D_MODEL, D_FF), D_MODEL),
        "w_mlp_out": nrm(ks[11], (DEPTH, D_FF, D_MODEL), D_FF),
        "w_in_a": nrm(ks[12], (N_A, D_MODEL, 2 * D_LRU + XA_WIDTH), D_MODEL),
        "conv_w": nrm(ks[13], (N_A, CONV_WIDTH, D_LRU), CONV_WIDTH),
        "conv_b": small(ks[14], (N_A, D_LRU)),
        "w_rgate": nrm(ks[15], (N_A, LRU_BLOCKS, LRU_BLK, LRU_BLK), LRU_BLK),
        "b_rgate": small(ks[16], (N_A, D_LRU)),
        "w_igate": nrm(ks[17], (N_A, LRU_BLOCKS, LRU_BLK, LRU_BLK), LRU_BLK),
        "b_igate": small(ks[18], (N_A, D_LRU)),
        "lru_lambda": lru_lambda,
        "w_in_b": nrm(ks[19], (N_B, D_MODEL, 3 * FOX_WIDTH + FOX_HEADS + XA_WIDTH), D_MODEL),
        "b_forget": jax.random.uniform(ks[20], (N_B, FOX_HEADS), f32, 1.0, 6.0),
        "fox_q_gain": gain(ks[21], (N_B, FOX_HD)),
        "fox_k_gain": gain(ks[22], (N_B, FOX_HD)),
    }


def reference(x, mem, norm_mix, norm_mem, w_mem_kv, xa_q_gain, xa_k_gain, w_out,
              norm_mlp, w_mlp_in, w_mlp_out, w_in_a, conv_w, conv_b, w_rgate, b_rgate,
              w_igate, b_igate, lru_lambda, w_in_b, b_forget, fox_q_gain, fox_k_gain):
    for layer in range(DEPTH):
        h = rms_norm(x, norm_mix[layer])
        j = layer // N_MIXERS
        if layer % N_MIXERS == 0:
            y_mix, xq = rglru_group(h, w_in_a[j], conv_w[j], conv_b[j], w_rgate[j], b_rgate[j],
                                    w_igate[j], b_igate[j], lru_lambda[j])
        else:
            y_mix, xq = fox_group(h, w_in_b[j], b_forget[j], fox_q_gain[j], fox_k_gain[j])
        y_mem = memory_group(xq, mem, norm_mem[layer], w_mem_kv[layer],
                             xa_q_gain[layer], xa_k_gain[layer])
        x = x + jnp.concatenate([y_mix, y_mem], axis=-1) @ w_out[layer]
        x = x + sqrelu_mlp(rms_norm(x, norm_mlp[layer]), w_mlp_in[layer], w_mlp_out[layer])
    return x
```

```python
import contextlib
import math
import numpy as np
import concourse.bass as bass
import concourse.mybir as mybir
from concourse.bass_utils import run_bass_kernel_spmd

F32 = mybir.dt.float32
BF16 = mybir.dt.bfloat16
AF = mybir.ActivationFunctionType
ALU = mybir.AluOpType
MUL, ADD, SUB = ALU.mult, ALU.add, ALU.subtract

ENGS = ["pe", "act", "dve", "pool", "sp"]
EPS = 1e-6
TT = 512
NPRO = 4
NL0 = 21
NL1 = 24
NPV = 104


class Op:
    __slots__ = ("eng", "fn", "reads", "writes", "dma", "deps", "signal", "idx", "sem", "val", "prev", "tag")

    def __init__(self, eng, fn, reads, writes, dma):
        self.eng = eng; self.fn = fn; self.reads = reads; self.writes = writes; self.dma = dma
        self.deps = set(); self.signal = False; self.sem = None; self.val = 0; self.prev = None


class Sched:
    def __init__(self, nc, n_dma_sems=8):
        self.nc = nc; self.ops = []; self.last_w = {}; self.readers = {}
        self.n_dma_sems = n_dma_sems; self.final_wait = []; self.phase = ''

    def add(self, eng, fn, reads=(), writes=(), dma=False, final=False):
        op = Op(eng, fn, tuple(reads), tuple(writes), dma)
        op.idx = len(self.ops); op.tag = self.phase
        for k in op.reads:
            w = self.last_w.get(k)
            if w is not None:
                op.deps.add(w)
        for k in op.writes:
            w = self.last_w.get(k)
            if w is not None:
                op.deps.add(w)
            for r in self.readers.get(k, ()):
                op.deps.add(r)
        for k in op.reads:
            self.readers.setdefault(k, []).append(op.idx)
        for k in op.writes:
            self.last_w[k] = op.idx
            self.readers[k] = []
        op.deps.discard(op.idx)
        self.ops.append(op)
        if final:
            self.final_wait.append(op.idx)
        return op

    def pe(self, fn, r=(), w=()): return self.add("pe", fn, r, w)
    def act(self, fn, r=(), w=()): return self.add("act", fn, r, w)
    def dve(self, fn, r=(), w=()): return self.add("dve", fn, r, w)
    def pool(self, fn, r=(), w=()): return self.add("pool", fn, r, w)
    def dma(self, fn, r=(), w=(), q="sp", final=False): return self.add(q, fn, r, w, dma=True, final=final)

    def emit(self):
        nc = self.nc; ops = self.ops
        for op in ops:
            nd = set()
            for d in op.deps:
                p = ops[d]
                if (not p.dma) and (not op.dma) and p.eng == "pe" and op.eng == "pe":
                    continue
                nd.add(d)
            best = {}
            nd2 = set()
            for d in nd:
                p = ops[d]
                if p.dma:
                    nd2.add(d)
                elif p.eng not in best or best[p.eng] < d:
                    best[p.eng] = d
            nd2.update(best.values())
            op.deps = nd2
            for d in nd2:
                ops[d].signal = True
        for i in self.final_wait:
            ops[i].signal = True
        with contextlib.ExitStack() as st:
            csem = {e: st.enter_context(nc.semaphore("c_" + e)) for e in ENGS}
            dsem = {e: [st.enter_context(nc.semaphore("d_%s_%d" % (e, i))) for i in range(self.n_dma_sems)]
                    for e in ("sp", "pool")}
            ccount = {e: 0 for e in ENGS}
            dcount = {e: [0] * self.n_dma_sems for e in dsem}
            drr = {e: 0 for e in dsem}
            prev_on_sem = {}
            for op in ops:
                if op.dma:
                    j = drr[op.eng] % self.n_dma_sems
                    drr[op.eng] += 1
                    op.sem = dsem[op.eng][j]
                    op.prev = prev_on_sem.get((op.eng, j))
                    dcount[op.eng][j] += 16
                    op.val = dcount[op.eng][j]
                    prev_on_sem[(op.eng, j)] = op
                elif op.signal:
                    ccount[op.eng] += 1
                    op.sem = csem[op.eng]
                    op.val = ccount[op.eng]
            per_eng = {e: [op for op in ops if op.eng == e] for e in ENGS}
            self.stats = {e: len(per_eng[e]) for e in ENGS}
            self.stats["sig"] = dict(ccount)
            self.tags = {e: [op.tag for op in per_eng[e]] for e in ENGS}
            block = st.enter_context(nc.Block())

            def make(e):
                def body(eng):
                    seen = {}
                    nwait = 0
                    for op in per_eng[e]:
                        need = {}
                        cands = [ops[d] for d in op.deps]
                        if op.dma and op.prev is not None:
                            cands.append(op.prev)
                        for p in cands:
                            key = id(p.sem)
                            if seen.get(key, 0) >= p.val:
                                continue
                            if key not in need or need[key][1] < p.val:
                                need[key] = (p.sem, p.val)
                        for key, (sem, val) in need.items():
                            eng.wait_ge(sem, val)
                            seen[key] = val
                            nwait += 1
                        ins = op.fn(eng)
                        if op.dma:
                            ins.then_inc(op.sem, 16)
                        elif op.signal:
                            ins.then_inc(op.sem, 1)
                    if e == "sp":
                        for i in self.final_wait:
                            p = ops[i]
                            if seen.get(id(p.sem), 0) < p.val:
                                eng.wait_ge(p.sem, p.val)
                                seen[id(p.sem)] = p.val
                    self.stats["wait_" + e] = nwait
                return body

            block.tensor(make("pe"))
            block.scalar(make("act"))
            block.vector(make("dve"))
            block.gpsimd(make("pool"))
            block.sync(make("sp"))


def MM(out, lhsT, rhs, start, stop):
    return lambda e: e.matmul(out, lhsT=lhsT, rhs=rhs, start=start, stop=stop)


def ACT(out, in_, func, scale=None, bias=None):
    kw = {}
    if scale is not None:
        kw["scale"] = scale
    if bias is not None:
        kw["bias"] = bias
    return lambda e: e.activation(out=out, in_=in_, func=func, **kw)


def STT(out, in0, scalar, in1, op0, op1):
    return lambda e: e.scalar_tensor_tensor(out=out, in0=in0, scalar=scalar, in1=in1, op0=op0, op1=op1)


def TS(out, in0, s1, op0, s2=None, op1=None):
    if op1 is None:
        return lambda e: e.tensor_scalar(out=out, in0=in0, scalar1=s1, scalar2=None, op0=op0)
    return lambda e: e.tensor_scalar(out=out, in0=in0, scalar1=s1, scalar2=s2, op0=op0, op1=op1)


def TTo(out, in0, in1, op):
    return lambda e: e.tensor_tensor(out=out, in0=in0, in1=in1, op=op)


def CP(out, in_):
    return lambda e: e.tensor_copy(out=out, in_=in_)


def RCP(out, in_):
    return lambda e: e.reciprocal_approx_fast(out=out, in_=in_)


def SCAN(out, d0, d1, init, op0, op1):
    return lambda e: e.tensor_tensor_scan(out=out, data0=d0, data1=d1, initial=init, op0=op0, op1=op1)


def DMA(out, in_):
    return lambda e: e.dma_start(out=out, in_=in_)


def build(NT=8, LAYERS=2, DBG=()):
    SL = NT * TT
    nc = bass.Bass("TRN2", target_bir_lowering=False)
    NBLK = NPRO + NL0 + NL1
    xT = nc.dram_tensor("xT", [1024, SL], F32, kind="ExternalInput").ap()
    memT = nc.dram_tensor("memT", [1024, 256], F32, kind="ExternalInput").ap()
    wst = nc.dram_tensor("wst", [NBLK, 128, 4096], F32, kind="ExternalInput").ap()
    wsm = nc.dram_tensor("wsm", [128, 1024], F32, kind="ExternalInput").ap()
    pv = nc.dram_tensor("pv", [128, NPV], F32, kind="ExternalInput").ap()
    yT = nc.dram_tensor("yT", [1024, SL], F32, kind="ExternalOutput").ap()
    wbf = nc.dram_tensor("wbf", [NBLK, 128, 4096], BF16, kind="Internal").ap()
    kd = nc.dram_tensor("kd", [8, 128, SL], BF16, kind="Internal").ap()
    vd = nc.dram_tensor("vd", [8, 128, SL // 128, 128], BF16, kind="Internal").ap()

    xTv = xT.rearrange("(c p) s -> p c s", p=128)
    yTv = yT.rearrange("(c p) s -> p c s", p=128)
    memTv = memT.rearrange("(c p) s -> p c s", p=128)

    with contextlib.ExitStack() as st:
        def T(name, shape, dt):
            return st.enter_context(nc.sbuf_tensor(name, shape, dt))

        NSLOT = 4
        ring = [T("ring%d" % i, [128, 4096], BF16) for i in range(NSLOT)]
        x = T("x", [128, 8, TT], F32)
        hn = T("hn", [128, 8, TT], BF16)
        ycat = T("ycat", [128, 8, TT], BF16)
        h1 = T("h1", [128, 16, TT], BF16)
        sq = h1
        GA = T("GA", [128, 4, TT], F32)
        ge4 = GA
        u = T("u", [128, 4, TT + 3], F32)
        xq = T("xq", [128, 4, TT], F32)
        pvs = T("pvs", [128, NPV], F32)
        dv = T("dv", [128, 32], F32)
        hstate = T("hstate", [128, 4], F32)
        carry = T("carry", [128, 8], F32)
        ones = T("ones", [128, TT], BF16)
        wg = T("wg", [128, 8, 128], BF16)
        memn = hn
        memf = xq[:].rearrange("p a (b c) -> p (a b) c", c=256)
        XQK = [("xq", m) for m in range(4)]
        kmT = T("kmT", [128, 2, 4, 256], BF16)
        vm = T("vm", [128, 2, 2, 512], BF16)
        NF = 12
        NB = 6
        fsl = [T("fs%d" % i, [128, TT], F32) for i in range(NF)]
        bsl = [T("bs%d" % i, [128, TT], BF16) for i in range(NB)]
        if LAYERS > 1:
            qa = GA[:].bitcast(BF16).rearrange("p a (b c) -> p (a b) c", c=TT)
            ka = T("ka", [128, 8, TT], BF16)
            va = T("va", [128, 8, 4, 128], BF16)
            kst = [T("kst%d" % i, [128, SL], BF16) for i in range(2)]
            vst = [T("vst%d" % i, [128, SL // 128, 128], BF16) for i in range(2)]
        psb = [st.enter_context(nc.psum_tensor("ps%d" % i, [128, TT], F32)) for i in range(8)]
        bgf = T("bgf", [128, TT], F32)
        bgb = [T("bgb%d" % i, [128, TT], BF16) for i in range(4)]

        S = Sched(nc)
        cnt = {"ps": 0, "f": 0, "b": 0}

        cnt["pl"] = 0

        def nextps():
            i = cnt["ps"] % 4
            cnt["ps"] += 1
            return psb[i], ("ps", i)

        def longps():
            i = 6 + cnt["pl"] % 2
            cnt["pl"] += 1
            return psb[i], ("ps", i)

        def tf():
            i = cnt["f"] % NF
            cnt["f"] += 1
            return fsl[i], ("f", i)

        def tb():
            i = cnt["b"] % NB
            cnt["b"] += 1
            return bsl[i], ("b", i)

        cast_done = set()

        def cast(i):
            if i in cast_done:
                return
            cast_done.add(i)
            S.dma(DMA(wbf[i].rearrange("p (a b) -> (p a) b", b=2048),
                      wst[i].rearrange("p (a b) -> (p a) b", b=2048)),
                  w=[("wbf", i)], q="pool")
        order = list(range(NPRO))
        for t in range(NT):
            order += list(range(NPRO, NPRO + NL0))
            if LAYERS > 1:
                order += list(range(NPRO + NL0, NBLK))
        rs = {"issued": 0, "got": 0, "rel": 0}

        def ring_issue():
            while rs["issued"] < len(order) and rs["issued"] - rs["rel"] < NSLOT:
                n = rs["issued"]
                for m_ in range(n, min(n + 8, len(order))):
                    cast(order[m_])
                slot = n % NSLOT
                S.dma(DMA(ring[slot][:], wbf[order[n]]), r=[("wbf", order[n])], w=[("ring", slot)])
                rs["issued"] += 1

        def ring_get(expect):
            n = rs["got"]
            assert order[n] == expect, (n, order[n], expect)
            assert n < rs["issued"]
            rs["got"] += 1
            slot = n % NSLOT
            return ring[slot], ("ring", slot)

        def ring_rel():
            rs["rel"] += 1
            ring_issue()

        S.pool(lambda e: e.memset(ones[:], 1.0), w=["ones"])
        S.pool(lambda e: e.memset(hstate[:], 0.0), w=["hstate"])
        S.pool(lambda e: e.memset(carry[:], 0.0), w=["carry"])
        S.pool(lambda e: e.memset(u[:, :, 0:3], 0.0), w=[("u", j) for j in range(4)])
        if LAYERS > 1:
            S.pool(lambda e: e.memset(va[:, :, :, 64:128], 1.0), w=["va1"])
        S.dma(DMA(pvs[:], pv), w=["pvs"])
        S.dma(DMA(wg[:].rearrange("p a b -> p (a b)"), wsm), w=["wg"], q="pool")
        S.dma(DMA(memf, memTv), w=XQK)
        ring_issue()
        S.act(ACT(dv[:, 0:4], pvs[:, 68:72], AF.Copy, scale=0.5), r=["pvs"], w=["dv"])
        S.act(ACT(dv[:, 4:8], pvs[:, 72:76], AF.Copy, scale=0.5), r=["pvs"], w=["dv"])
        S.act(ACT(dv[:, 28:32], pvs[:, 76:80], AF.Exp, scale=-1.0), r=["pvs"], w=["dv"])
        S.act(ACT(dv[:, 28:32], dv[:, 28:32], AF.Ln, bias=1.0), r=["dv"], w=["dv"])
        S.act(ACT(dv[:, 8:12], dv[:, 28:32], AF.Copy, scale=-8.0), r=["dv"], w=["dv"])
        S.act(ACT(dv[:, 12:16], dv[:, 28:32], AF.Copy, scale=-4.0), r=["dv"], w=["dv"])
        S.act(ACT(dv[:, 16:17], pvs[:, 80:81], AF.Copy, scale=1.0 / math.sqrt(128.0)), r=["pvs"], w=["dv"])
        S.act(ACT(dv[:, 17:18], pvs[:, 82:83], AF.Copy, scale=1.0 / math.sqrt(128.0)), r=["pvs"], w=["dv"])
        S.act(ACT(dv[:, 18:19], pvs[:, 84:85], AF.Copy, scale=0.125), r=["pvs"], w=["dv"])
        S.act(ACT(dv[:, 20:28], pvs[:, 86:94], AF.Copy, scale=-1.0), r=["pvs"], w=["dv"])

        for l in range(2):
            if l >= LAYERS:
                wk, wkk = ring_get(2 * l); ring_rel()
                wv, wvk = ring_get(2 * l + 1); ring_rel()
                continue
            for c in range(8):
                S.act(ACT(sq[:, c, 0:256], memf[:, c, :], AF.Square), r=XQK, w=[("h1", c)])
            pa, pk = nextps()
            for c in range(8):
                S.pe(MM(pa[:, 0:256], ones[:, 0:128], sq[:, c, 0:256], c == 0, c == 7), r=[("h1", c), "ones"], w=[pk])
            sd, sdk = tf()
            S.act(ACT(sd[:, 0:256], pa[:, 0:256], AF.Ln, scale=1.0 / 1024.0, bias=EPS), r=[pk], w=[sdk])
            S.act(ACT(sd[:, 0:256], sd[:, 0:256], AF.Exp, scale=-0.5), r=[sdk], w=[sdk])
            for c in range(8):
                S.dve(STT(memn[:, c, 0:256], memf[:, c, :], pvs[:, 32 + 8 * l + c:33 + 8 * l + c], sd[:, 0:256], MUL, MUL),
                      r=XQK + [sdk, "pvs"], w=[("hn", c)])
            wk, wkk = ring_get(2 * l)
            for m in range(4):
                pa, pk = nextps()
                for c in range(8):
                    S.pe(MM(pa[:, 0:256], wk[:, c * 512 + m * 128:c * 512 + (m + 1) * 128], memn[:, c, 0:256], c == 0, c == 7),
                         r=[wkk, ("hn", c)], w=[pk])
                sqb, sqk = tb()
                S.act(ACT(sqb[:, 0:256], pa[:, 0:256], AF.Square), r=[pk], w=[sqk])
                pb, pbk = nextps()
                S.pe(MM(pb[:, 0:256], ones[:, 0:128], sqb[:, 0:256], True, True), r=[sqk, "ones"], w=[pbk])
                sd, sdk = tf()
                S.act(ACT(sd[:, 0:256], pb[:, 0:256], AF.Ln, scale=1.0 / 128.0, bias=EPS), r=[pbk], w=[sdk])
                S.act(ACT(sd[:, 0:256], sd[:, 0:256], AF.Exp, scale=-0.5), r=[sdk], w=[sdk])
                S.dve(STT(kmT[:, l, m, :], pa[:, 0:256], pvs[:, 81 + 2 * l:82 + 2 * l], sd[:, 0:256], MUL, MUL),
                      r=[pk, sdk, "pvs"], w=[("kmT", l)])
            ring_rel()
            wv, wvk = ring_get(2 * l + 1)
            for mc in range(2):
                pa, pk = nextps()
                for c in range(8):
                    S.pe(MM(pa[:], memn[:, c, mc * 128:(mc + 1) * 128], wv[:, c * 512:(c + 1) * 512], c == 0, c == 7),
                         r=[wvk, ("hn", c)], w=[pk])
                S.act(ACT(vm[:, l, mc, :], pa[:], AF.Copy), r=[pk], w=[("vm", l)])
            ring_rel()

        def rmsnorm(gcol):
            S.phase = 'norm'
            for c in range(8):
                S.act(ACT(sq[:, c, :], x[:, c, :], AF.Square), r=[("x", c)], w=[("h1", c)])
            pa, pk = nextps()
            for c in range(8):
                S.pe(MM(pa[:], ones[:, 0:128], sq[:, c, :], c == 0, c == 7), r=[("h1", c), "ones"], w=[pk])
            sd, sdk = tf()
            S.act(ACT(sd[:], pa[:], AF.Ln, scale=1.0 / 1024.0, bias=EPS), r=[pk], w=[sdk])
            S.act(ACT(sd[:], sd[:], AF.Exp, scale=-0.5), r=[sdk], w=[sdk])
            for c in range(8):
                S.dve(STT(hn[:, c, :], x[:, c, :], pvs[:, gcol + c:gcol + c + 1], sd[:], MUL, MUL),
                      r=[("x", c), sdk, "pvs"], w=[("hn", c)])

        bg = []

        def mem_head(l, m):
            A, Ak = psb[4], ("ps", 4)
            B, Bk = psb[5], ("ps", 5)
            bsq, bqn, bp0, bp1 = bgb
            S.phase = 'memattn'
            S.act(ACT(bsq[:], xq[:, m, :], AF.Square), r=[("xq", m)], w=["bsq"])
            yield
            S.phase = 'memattn'
            S.pe(MM(A[:], ones[:, 0:128], bsq[:], True, True), r=["bsq", "ones"], w=[Ak])
            yield
            S.phase = 'memattn'
            S.act(ACT(bgf[:], A[:], AF.Ln, scale=1.0 / 128.0, bias=EPS), r=[Ak], w=["bgf"])
            S.act(ACT(bgf[:], bgf[:], AF.Exp, scale=-0.5), r=["bgf"], w=["bgf"])
            yield
            S.phase = 'memattn'
            S.dve(STT(bqn[:], xq[:, m, :], dv[:, 16 + l:17 + l], bgf[:], MUL, MUL), r=[("xq", m), "bgf", "dv"], w=["bqn"])
            yield
            S.phase = 'memattn'
            S.pe(MM(A[:], kmT[:, l, m, 0:128], bqn[:], True, True), r=[("kmT", l), "bqn"], w=[Ak])
            S.pe(MM(B[:], kmT[:, l, m, 128:256], bqn[:], True, True), r=[("kmT", l), "bqn"], w=[Bk])
            yield
            S.phase = 'memattn'
            S.act(ACT(bp0[:], A[:], AF.Exp), r=[Ak], w=["bp0"])
            S.act(ACT(bp1[:], B[:], AF.Exp), r=[Bk], w=["bp1"])
            yield
            S.phase = 'memattn'
            S.pe(MM(A[:], vm[:, l, 0, m * 128:(m + 1) * 128], bp0[:], True, False), r=[("vm", l), "bp0"], w=[Ak])
            S.pe(MM(A[:], vm[:, l, 1, m * 128:(m + 1) * 128], bp1[:], False, True), r=[("vm", l), "bp1"], w=[Ak])
            S.pe(MM(B[:], ones[:, 0:128], bp0[:], True, False), r=["ones", "bp0"], w=[Bk])
            S.pe(MM(B[:], ones[:, 0:128], bp1[:], False, True), r=["ones", "bp1"], w=[Bk])
            yield
            S.phase = 'memattn'
            S.act(ACT(bgf[:], B[:], AF.Ln), r=[Bk], w=["bgf"])
            S.act(ACT(bgf[:], bgf[:], AF.Exp, scale=-1.0), r=["bgf"], w=["bgf"])
            yield
            S.phase = 'memattn'
            S.dve(TTo(ycat[:, 4 + m, :], A[:], bgf[:], MUL), r=[Ak, "bgf"], w=[("ycat", 4 + m)])

        def bg_step(n=1):
            ph = S.phase
            for _ in range(n):
                if not bg:
                    break
                try:
                    next(bg[0])
                except StopIteration:
                    bg.pop(0)
            S.phase = ph

        def bg_flush():
            while bg:
                bg_step()

        def xq_proj(blk):
            wx, wxk = ring_get(blk)
            for m in range(4):
                pa, pk = nextps()
                for c in range(8):
                    S.pe(MM(pa[:], wx[:, c * 512 + m * 128:c * 512 + (m + 1) * 128], hn[:, c, :], c == 0, c == 7),
                         r=[wxk, ("hn", c)], w=[pk])
                S.act(ACT(xq[:, m, :], pa[:], AF.Copy), r=[pk], w=[("xq", m)])
            ring_rel()

        def out_proj_mlp(l, base):
            S.phase = 'outproj'
            for half in range(2):
                wo, wok = ring_get(base + half)
                for ff in range(4):
                    f = half * 4 + ff
                    pa, pk = nextps()
                    for c in range(8):
                        S.pe(MM(pa[:], wo[:, c * 512 + ff * 128:c * 512 + (ff + 1) * 128], ycat[:, c, :], c == 0, c == 7),
                             r=[wok, ("ycat", c)], w=[pk])
                    S.dve(TTo(x[:, f, :], x[:, f, :], pa[:], ADD), r=[pk, ("x", f)], w=[("x", f)])
                ring_rel()
            rmsnorm(8 + 16 * l)
            S.phase = 'mlp'
            for hh in range(2):
                for g in range(4):
                    w1, w1k = ring_get(base + 2 + hh * 8 + g)
                    for j in range(4):
                        pa, pk = nextps()
                        for c in range(8):
                            S.pe(MM(pa[:], w1[:, c * 512 + j * 128:c * 512 + (j + 1) * 128], hn[:, c, :], c == 0, c == 7),
                                 r=[w1k, ("hn", c)], w=[pk])
                        rl, rlk = tf()
                        S.act(ACT(rl[:], pa[:], AF.Relu), r=[pk], w=[rlk])
                        S.pool(TTo(h1[:, g * 4 + j, :], rl[:], rl[:], MUL), r=[rlk], w=[("h1", g * 4 + j)])
                    ring_rel()
                for fp in range(4):
                    w2, w2k = ring_get(base + 2 + hh * 8 + 4 + fp)
                    for ff in range(2):
                        f = fp * 2 + ff
                        pa, pk = nextps()
                        for k in range(16):
                            S.pe(MM(pa[:], w2[:, k * 256 + ff * 128:k * 256 + (ff + 1) * 128], h1[:, k, :], k == 0, k == 15),
                                 r=[w2k, ("h1", k)], w=[pk])
                        S.dve(TTo(x[:, f, :], x[:, f, :], pa[:], ADD), r=[pk, ("x", f)], w=[("x", f)])
                    ring_rel()

        C0 = math.sqrt(2.0 / math.pi)

        def layer0(t):
            base = NPRO
            rmsnorm(0)
            S.phase = 'l0.inproj'
            xq_proj(base + 0)
            for m in range(4):
                bg.append(mem_head(0, m))
            wu, wuk = ring_get(base + 1)
            for j in range(4):
                pa, pk = nextps()
                for c in range(8):
                    S.pe(MM(pa[:], wu[:, c * 512 + j * 128:c * 512 + (j + 1) * 128], hn[:, c, :], c == 0, c == 7),
                         r=[wuk, ("hn", c)], w=[pk])
                S.act(ACT(u[:, j, 3:TT + 3], pa[:], AF.Copy), r=[pk], w=[("u", j)])
                bg_step(2)
            ring_rel()
            wgt, wgk = ring_get(base + 2)
            for j in range(4):
                pa, pk = nextps()
                for c in range(8):
                    S.pe(MM(pa[:], wgt[:, c * 512 + j * 128:c * 512 + (j + 1) * 128], hn[:, c, :], c == 0, c == 7),
                         r=[wgk, ("hn", c)], w=[pk])
                gs, gsk = tf()
                g2, g2k = tf()
                gek = [("qa", 2 * j), ("qa", 2 * j + 1)]
                S.act(ACT(gs[:], pa[:], AF.Copy), r=[pk], w=[gsk])
                S.act(ACT(g2[:], pa[:], AF.Square), r=[pk], w=[g2k])
                S.dve(TS(g2[:], g2[:], 0.044715, MUL, 1.0, ADD), r=[g2k], w=[g2k])
                S.dve(TTo(g2[:], g2[:], gs[:], MUL), r=[g2k, gsk], w=[g2k])
                S.act(ACT(g2[:], g2[:], AF.Tanh, scale=C0), r=[g2k], w=[g2k])
                S.dve(STT(ge4[:, j, :], g2[:], 1.0, gs[:], ADD, MUL), r=[g2k, gsk], w=gek)
                bg_step(2)
            ring_rel()
            S.phase = 'l0.rec'
            for j in range(4):
                gek = [("qa", 2 * j), ("qa", 2 * j + 1)]
                uc, uck = tf()
                S.dve(TS(uc[:], u[:, j, 3:TT + 3], pvs[:, 60 + j:61 + j], MUL, pvs[:, 64 + j:65 + j], ADD),
                      r=[("u", j), "pvs"], w=[uck])
                for tap in (2, 1, 0):
                    S.dve(STT(uc[:], u[:, j, tap:tap + TT], pvs[:, 48 + 4 * tap + j:49 + 4 * tap + j], uc[:], MUL, ADD),
                          r=[("u", j), "pvs", uck], w=[uck])
                S.pool(CP(u[:, j, 0:3], u[:, j, TT:TT + 3]), r=[("u", j)], w=[("u", j)])
                ucb, ucbk = tb()
                S.act(ACT(ucb[:], uc[:], AF.Copy), r=[uck], w=[ucbk])
                pr, prk = nextps()
                S.pe(MM(pr[:], wg[:, j, :], ucb[:], True, True), r=["wg", ucbk], w=[prk])
                pi, pik = nextps()
                S.pe(MM(pi[:], wg[:, 4 + j, :], ucb[:], True, True), r=["wg", ucbk], w=[pik])
                tr, trk = tf()
                S.act(ACT(tr[:], pr[:], AF.Tanh, scale=0.5, bias=dv[:, j:j + 1]), r=[prk, "dv"], w=[trk])
                av, avk = tf()
                S.act(ACT(av[:], tr[:], AF.Exp, scale=dv[:, 12 + j:13 + j], bias=dv[:, 12 + j:13 + j]), r=[trk, "dv"], w=[avk])
                a2, a2k = tf()
                S.act(ACT(a2[:], tr[:], AF.Exp, scale=dv[:, 8 + j:9 + j], bias=dv[:, 8 + j:9 + j]), r=[trk, "dv"], w=[a2k])
                ti, tik = tf()
                S.act(ACT(ti[:], pi[:], AF.Tanh, scale=0.5, bias=dv[:, 4 + j:5 + j]), r=[pik, "dv"], w=[tik])
                S.dve(STT(ti[:], ti[:], 1.0, uc[:], ADD, MUL), r=[tik, uck], w=[tik])
                S.act(ACT(a2[:], a2[:], AF.Ln, scale=-1.0, bias=1.0), r=[a2k], w=[a2k])
                S.act(ACT(a2[:], a2[:], AF.Exp, scale=0.5), r=[a2k], w=[a2k])
                S.dve(STT(ti[:], ti[:], 0.5, a2[:], MUL, MUL), r=[tik, a2k], w=[tik])
                hs, hsk = tf()
                S.dve(SCAN(hs[:], av[:], ti[:], hstate[:, j:j + 1], MUL, ADD), r=[avk, tik, "hstate"], w=[hsk])
                S.dve(CP(hstate[:, j:j + 1], hs[:, TT - 1:TT]), r=[hsk], w=["hstate"])
                S.dve(STT(ycat[:, j, :], hs[:], 0.5, ge4[:, j, :], MUL, MUL), r=[hsk] + gek, w=[("ycat", j)])
                bg_step(3)
            bg_flush()
            out_proj_mlp(0, base + 3)

        def layer1(t):
            base = NPRO + NL0
            rmsnorm(16)
            S.phase = 'l1.qk'
            xq_proj(base + 0)
            for m in range(4):
                bg.append(mem_head(1, m))
            S.phase = 'l1.qk'
            R = slice(64, 70)

            def q_tail(h, pa, pk, sqb, sqk):
                pb, pbk = nextps()
                S.pe(MM(pb[0:64, :], ones[0:64, 0:64], sqb[0:64, :], True, True), r=[sqk, "ones"], w=[pbk])
                sd, sdk = tf()
                S.act(ACT(sd[0:64, :], pb[0:64, :], AF.Ln, scale=1.0 / 64.0, bias=EPS), r=[pbk], w=[sdk])
                S.act(ACT(sd[0:64, :], sd[0:64, :], AF.Exp, scale=-0.5), r=[sdk], w=[sdk])
                S.dve(STT(qa[0:64, h, :], pa[0:64, :], dv[0:64, 18:19], sd[0:64, :], MUL, MUL),
                      r=[pk, sdk, "dv"], w=[("qa", h)])
                e1, e1k = tf()
                S.act(ACT(e1[R, :], pa[R, :], AF.Exp, scale=-1.0, bias=dv[R, 20 + h:21 + h]), r=[pk, "dv"], w=[e1k])
                S.act(ACT(e1[R, :], e1[R, :], AF.Ln, bias=1.0), r=[e1k], w=[e1k])
                cs, csk = tf()
                S.dve(SCAN(cs[R, :], ones[R, :], e1[R, :], carry[R, h:h + 1], MUL, SUB), r=[e1k, "ones", "carry"], w=[csk])
                S.pool(CP(carry[R, h:h + 1], cs[R, TT - 1:TT]), r=[csk], w=["carry"])
                t1, t1k = tb()
                S.dve(CP(t1[R, :], cs[R, :]), r=[csk], w=[t1k])
                t2, t2k = tf()
                S.dve(STT(t2[R, :], t1[R, :], pvs[R, 94:95], cs[R, :], MUL, ADD), r=[t1k, csk, "pvs"], w=[t2k])
                t3, t3k = tb()
                S.dve(CP(t3[R, :], t2[R, :]), r=[t2k], w=[t3k])
                S.dve(STT(t2[R, :], t3[R, :], pvs[R, 95:96], t2[R, :], MUL, ADD), r=[t3k, t2k, "pvs"], w=[t2k])
                S.pool(TS(qa[R, h, :], t2[R, :], pvs[R, 96:97], MUL, pvs[R, 97:98], ADD), r=[t2k, "pvs"], w=[("qa", h)])
                S.pool(TS(ka[R, h, :], t2[R, :], pvs[R, 98:99], MUL, pvs[R, 99:100], ADD), r=[t2k, "pvs"], w=[("ka", h)])

            def k_tail(h, pa, pk, sqb, sqk):
                pb, pbk = nextps()
                S.pe(MM(pb[0:64, :], ones[0:64, 0:64], sqb[0:64, :], True, True), r=[sqk, "ones"], w=[pbk])
                sd, sdk = tf()
                S.act(ACT(sd[0:64, :], pb[0:64, :], AF.Ln, scale=1.0 / 64.0, bias=EPS), r=[pbk], w=[sdk])
                S.act(ACT(sd[0:64, :], sd[0:64, :], AF.Exp, scale=-0.5), r=[sdk], w=[sdk])
                S.dve(STT(ka[0:64, h, :], pa[0:64, :], pvs[0:64, 85:86], sd[0:64, :], MUL, MUL),
                      r=[pk, sdk, "pvs"], w=[("ka", h)])

            pend = None
            for blk in range(4):
                ww, wwk = ring_get(base + 1 + blk)
                isq = blk < 2
                M = 70 if isq else 64
                for hh in range(4):
                    h = (blk % 2) * 4 + hh
                    pa, pk = nextps()
                    for c in range(8):
                        S.pe(MM(pa[0:M, :], ww[:, c * 512 + hh * 128:c * 512 + hh * 128 + M], hn[:, c, :], c == 0, c == 7),
                             r=[wwk, ("hn", c)], w=[pk])
                    sqb, sqk = tb()
                    S.act(ACT(sqb[0:64, :], pa[0:64, :], AF.Square), r=[pk], w=[sqk])
                    if pend is not None:
                        pend[0](*pend[1])
                    pend = (q_tail if isq else k_tail, (h, pa, pk, sqb, sqk))
                    bg_step(1)
                ring_rel()
            pend[0](*pend[1])
            S.phase = 'l1.v_xq'
            wv, wvk = ring_get(base + 5)
            for tc in range(4):
                pa, pk = nextps()
                for c in range(8):
                    S.pe(MM(pa[:], hn[:, c, tc * 128:(tc + 1) * 128], wv[:, c * 512:(c + 1) * 512], c == 0, c == 7),
                         r=[wvk, ("hn", c)], w=[pk])
                S.act(ACT(va[:, :, tc, 0:64], pa[:].rearrange("p (h d) -> p h d", d=64), AF.Copy), r=[pk], w=[("va", tc)])
                bg_step(1)
            ring_rel()
            if t + 1 < NT and 'nostore' not in DBG:
                for h in range(8):
                    S.dma(DMA(kd[h, :, t * TT:(t + 1) * TT], ka[:, h, :]), r=[("ka", h)], w=[("kd", t, h)])
                    S.dma(DMA(vd[h, :, t * 4:(t + 1) * 4, :], va[:, h, :, :]),
                          r=[("va", tc) for tc in range(4)] + ["va1"], w=[("vd", t, h)])
            S.phase = 'l1.attn'
            for h in range(8):
                so = h % 2
                if t > 0 and 'noload' not in DBG:
                    S.dma(DMA(kst[so][:, 0:t * TT], kd[h, :, 0:t * TT]), r=[("kd", tt, h) for tt in range(t)], w=[("kst", so)])
                    S.dma(DMA(vst[so][:, 0:t * 4, :], vd[h, :, 0:t * 4, :]), r=[("vd", tt, h) for tt in range(t)], w=[("vst", so)])
                po, pok = longps()
                nkc = 4 * (t + 1)
                seq = [4 * t + d for d in range(4)] + list(range(4 * t))
                if 'noold' in DBG:
                    seq = seq[:4]; nkc = 4
                LOOK = 2
                qk = []

                def issue_qk(kc):
                    if kc >= 4 * t:
                        d = kc - 4 * t
                        qlo = d * 128
                        kl = ka[0:70, h, d * 128:(d + 1) * 128]
                        vl = va[:, h, d, :]
                        kr = [("ka", h)]
                        vr = [("va", d), "va1"]
                    else:
                        d = -1
                        qlo = 0
                        kl = kst[so][0:70, kc * 128:(kc + 1) * 128]
                        vl = vst[so][:, kc, :]
                        kr = [("kst", so)]
                        vr = [("vst", so)]
                    ps_, psk = nextps()
                    S.pe(MM(ps_[:, qlo:TT], kl, qa[0:70, h, qlo:TT], True, True), r=kr + [("qa", h)], w=[psk])
                    qk.append((ps_, psk, qlo, d, vl, vr))

                for n in range(len(seq)):
                    while len(qk) < min(n + 1 + LOOK, len(seq)):
                        issue_qk(seq[len(qk)])
                    ps_, psk, qlo, d, vl, vr = qk[n]
                    pt, ptk = tb()
                    S.act(ACT(pt[:, qlo:TT], ps_[:, qlo:TT], AF.Exp), r=[psk], w=[ptk])
                    if d >= 0:
                        S.pool(lambda e, pt=pt, qlo=qlo: e.affine_select(
                            out=pt[:, qlo:qlo + 128], in_=pt[:, qlo:qlo + 128], pattern=[[1, 128]],
                            compare_op=ALU.is_ge, fill=0.0, base=0, channel_multiplier=-1), r=[ptk], w=[ptk])
                    S.pe(MM(po[:, qlo:TT], vl, pt[:, qlo:TT], n == 0, n == nkc - 1), r=vr + [ptk], w=[pok])
                    if n % 2 == 1:
                        bg_step(1)
                rd, rdk = tf()
                S.act(ACT(rd[0:64, :], po[64:128, :], AF.Ln), r=[pok], w=[rdk])
                S.act(ACT(rd[0:64, :], rd[0:64, :], AF.Exp, scale=-1.0), r=[rdk], w=[rdk])
                po_ = (h % 2) * 64
                S.dve(TTo(ycat[po_:po_ + 64, h // 2, :], po[0:64, :], rd[0:64, :], MUL), r=[pok, rdk], w=[("ycat", h // 2)])
            bg_flush()
            out_proj_mlp(1, base + 6)

        for t in range(NT):
            S.dma(DMA(x[:], xTv[:, :, t * TT:(t + 1) * TT]), w=[("x", c) for c in range(8)])
            layer0(t)
            if LAYERS > 1:
                layer1(t)
            S.dma(DMA(yTv[:, :, t * TT:(t + 1) * TT], x[:]), r=[("x", c) for c in range(8)], final=True)
        assert rs["got"] == len(order), (rs, len(order))
        S.emit()
        build.stats = S.stats
        build.tags = S.tags
    return nc


def _kblock(w):
    return np.ascontiguousarray(w.reshape(8, 128, 512).transpose(1, 0, 2)).reshape(128, 4096)


def _pcol(v, n):
    return np.ascontiguousarray(np.asarray(v).reshape(n, 128).T)


def prepare(inp, LAYERS=2):
    f32 = np.float32
    blocks = []
    for l in range(2):
        blocks.append(_kblock(inp["w_mem_kv"][l][:, 0:512]))
        blocks.append(_kblock(inp["w_mem_kv"][l][:, 512:1024]))

    def mlp_blocks(l):
        out = []
        w1 = inp["w_mlp_in"][l]; w2 = inp["w_mlp_out"][l]
        for hh in range(2):
            for g in range(4):
                out.append(_kblock(w1[:, (hh * 4 + g) * 512:(hh * 4 + g + 1) * 512]))
            for fp in range(4):
                blk = w2[hh * 2048:(hh + 1) * 2048, fp * 256:(fp + 1) * 256]
                out.append(np.ascontiguousarray(blk.reshape(16, 128, 256).transpose(1, 0, 2)).reshape(128, 4096))
        return out

    wa = inp["w_in_a"][0]
    for part in (2, 0, 1):
        blocks.append(_kblock(wa[:, part * 512:(part + 1) * 512]))
    for half in range(2):
        blocks.append(_kblock(inp["w_out"][0][:, half * 512:(half + 1) * 512]))
    blocks += mlp_blocks(0)
    wb = inp["w_in_b"][0]
    blocks.append(_kblock(wb[:, 1544:2056]))
    for jb in range(2):
        blk = np.zeros((1024, 4, 128), f32)
        for hh in range(4):
            h = jb * 4 + hh
            blk[:, hh, 0:64] = wb[:, h * 64:(h + 1) * 64]
            blk[:, hh, 64:70] = wb[:, 1536 + h:1537 + h]
        blocks.append(_kblock(blk.reshape(1024, 512)))
    for jb in range(2):
        blk = np.zeros((1024, 4, 128), f32)
        for hh in range(4):
            h = jb * 4 + hh
            blk[:, hh, 0:64] = wb[:, 512 + h * 64:512 + (h + 1) * 64]
        blocks.append(_kblock(blk.reshape(1024, 512)))
    blocks.append(_kblock(wb[:, 1024:1536]))
    for half in range(2):
        blocks.append(_kblock(inp["w_out"][1][:, half * 512:(half + 1) * 512]))
    blocks += mlp_blocks(1)
    wst = np.stack(blocks).astype(f32)
    assert wst.shape[0] == NPRO + NL0 + NL1, wst.shape

    wsm = np.zeros((128, 8, 128), f32)
    for gi, wgate in enumerate((inp["w_rgate"][0], inp["w_igate"][0])):
        for j in range(4):
            for b in range(2):
                wsm[b * 64:(b + 1) * 64, gi * 4 + j, b * 64:(b + 1) * 64] = wgate[2 * j + b]
    wsm = wsm.reshape(128, 1024)

    pvv = np.zeros((128, NPV), f32)
    pvv[:, 0:8] = _pcol(inp["norm_mix"][0], 8)
    pvv[:, 8:16] = _pcol(inp["norm_mlp"][0], 8)
    pvv[:, 16:24] = _pcol(inp["norm_mix"][1], 8)
    pvv[:, 24:32] = _pcol(inp["norm_mlp"][1], 8)
    pvv[:, 32:40] = _pcol(inp["norm_mem"][0], 8)
    pvv[:, 40:48] = _pcol(inp["norm_mem"][1], 8)
    for tap in range(4):
        pvv[:, 48 + 4 * tap:52 + 4 * tap] = _pcol(inp["conv_w"][0][tap], 4)
    pvv[:, 64:68] = _pcol(inp["conv_b"][0], 4)
    pvv[:, 68:72] = _pcol(inp["b_rgate"][0], 4)
    pvv[:, 72:76] = _pcol(inp["b_igate"][0], 4)
    pvv[:, 76:80] = _pcol(inp["lru_lambda"][0], 4)
    pvv[:, 80] = inp["xa_q_gain"][0]; pvv[:, 81] = inp["xa_k_gain"][0]
    pvv[:, 82] = inp["xa_q_gain"][1]; pvv[:, 83] = inp["xa_k_gain"][1]
    pvv[0:64, 84] = inp["fox_q_gain"][0]; pvv[64:128, 84] = inp["fox_q_gain"][0]
    pvv[0:64, 85] = inp["fox_k_gain"][0]; pvv[64:128, 85] = inp["fox_k_gain"][0]
    pvv[:, 86:94] = np.asarray(inp["b_forget"][0])[None, :]
    pvv[[65, 66, 68, 69], 94] = -1.0
    pvv[[66, 69], 95] = -1.0
    pvv[64:67, 96] = 1.0; pvv[67:70, 97] = 1.0
    pvv[67:70, 98] = -1.0; pvv[64:67, 99] = 1.0
    return wst, wsm, pvv


_NC_CACHE = {}


def kernel(**inputs):
    inp = {k: np.asarray(v) for k, v in inputs.items()}
    B, SEQ, D = inp["x"].shape
    NT = SEQ // TT
    wst, wsm, pvv = prepare(inp)
    key = (NT, 2)
    if key not in _NC_CACHE:
        _NC_CACHE[key] = build(NT, 2)
    nc = _NC_CACHE[key]
    in_maps = []
    for b in range(B):
        in_maps.append({
            "xT": np.ascontiguousarray(inp["x"][b].T.astype(np.float32)),
            "memT": np.ascontiguousarray(inp["mem"][b].T.astype(np.float32)),
            "wst": wst, "wsm": wsm, "pv": pvv,
        })
    res = run_bass_kernel_spmd(nc, in_maps, core_ids=list(range(B)))
    out = np.stack([np.ascontiguousarray(r["yT"].T) for r in res.results]).astype(np.float32)
    return out
```

```python
import contextlib
import math
import numpy as np
import concourse.bass as bass
import concourse.mybir as mybir
from concourse.bass_utils import run_bass_kernel_spmd

F32 = mybir.dt.float32
BF16 = mybir.dt.bfloat16
AF = mybir.ActivationFunctionType
ALU = mybir.AluOpType
MUL, ADD, SUB = ALU.mult, ALU.add, ALU.subtract

ENGS = ["pe", "act", "dve", "pool", "sp"]
EPS = 1e-6
TT = 512
NPRO = 4
NL0 = 21
NL1 = 24
NPV = 104


class Op:
    __slots__ = ("eng", "fn", "reads", "writes", "dma", "deps", "signal", "idx", "sem", "val", "prev", "tag")

    def __init__(self, eng, fn, reads, writes, dma):
        self.eng = eng; self.fn = fn; self.reads = reads; self.writes = writes; self.dma = dma
        self.deps = set(); self.signal = False; self.sem = None; self.val = 0; self.prev = None


class Sched:
    def __init__(self, nc, n_dma_sems=8):
        self.nc = nc; self.ops = []; self.last_w = {}; self.readers = {}
        self.n_dma_sems = n_dma_sems; self.final_wait = []; self.phase = ''

    def add(self, eng, fn, reads=(), writes=(), dma=False, final=False):
        op = Op(eng, fn, tuple(reads), tuple(writes), dma)
        op.idx = len(self.ops); op.tag = self.phase
        for k in op.reads:
            w = self.last_w.get(k)
            if w is not None:
                op.deps.add(w)
        for k in op.writes:
            w = self.last_w.get(k)
            if w is not None:
                op.deps.add(w)
            for r in self.readers.get(k, ()):
                op.deps.add(r)
        for k in op.reads:
            self.readers.setdefault(k, []).append(op.idx)
        for k in op.writes:
            self.last_w[k] = op.idx
            self.readers[k] = []
        op.deps.discard(op.idx)
        self.ops.append(op)
        if final:
            self.final_wait.append(op.idx)
        return op

    def pe(self, fn, r=(), w=()): return self.add("pe", fn, r, w)
    def act(self, fn, r=(), w=()): return self.add("act", fn, r, w)
    def dve(self, fn, r=(), w=()): return self.add("dve", fn, r, w)
    def pool(self, fn, r=(), w=()): return self.add("pool", fn, r, w)
    def dma(self, fn, r=(), w=(), q="sp", final=False): return self.add(q, fn, r, w, dma=True, final=final)

    def emit(self):
        nc = self.nc; ops = self.ops
        for op in ops:
            nd = set()
            for d in op.deps:
                p = ops[d]
                if (not p.dma) and (not op.dma) and p.eng == "pe" and op.eng == "pe":
                    continue
                nd.add(d)
            best = {}
            nd2 = set()
            for d in nd:
                p = ops[d]
                if p.dma:
                    nd2.add(d)
                elif p.eng not in best or best[p.eng] < d:
                    best[p.eng] = d
            nd2.update(best.values())
            op.deps = nd2
            for d in nd2:
                ops[d].signal = True
        for i in self.final_wait:
            ops[i].signal = True
        with contextlib.ExitStack() as st:
            csem = {e: st.enter_context(nc.semaphore("c_" + e)) for e in ENGS}
            dsem = {e: [st.enter_context(nc.semaphore("d_%s_%d" % (e, i))) for i in range(self.n_dma_sems)]
                    for e in ("sp", "pool")}
            ccount = {e: 0 for e in ENGS}
            dcount = {e: [0] * self.n_dma_sems for e in dsem}
            drr = {e: 0 for e in dsem}
            prev_on_sem = {}
            for op in ops:
                if op.dma:
                    j = drr[op.eng] % self.n_dma_sems
                    drr[op.eng] += 1
                    op.sem = dsem[op.eng][j]
                    op.prev = prev_on_sem.get((op.eng, j))
                    dcount[op.eng][j] += 16
                    op.val = dcount[op.eng][j]
                    prev_on_sem[(op.eng, j)] = op
                elif op.signal:
                    ccount[op.eng] += 1
                    op.sem = csem[op.eng]
                    op.val = ccount[op.eng]
            per_eng = {e: [op for op in ops if op.eng == e] for e in ENGS}
            self.stats = {e: len(per_eng[e]) for e in ENGS}
            self.stats["sig"] = dict(ccount)
            self.tags = {e: [op.tag for op in per_eng[e]] for e in ENGS}
            block = st.enter_context(nc.Block())

            def make(e):
                def body(eng):
                    seen = {}
                    nwait = 0
                    for op in per_eng[e]:
                        need = {}
                        cands = [ops[d] for d in op.deps]
                        if op.dma and op.prev is not None:
                            cands.append(op.prev)
                        for p in cands:
                            key = id(p.sem)
                            if seen.get(key, 0) >= p.val:
                                continue
                            if key not in need or need[key][1] < p.val:
                                need[key] = (p.sem, p.val)
                        for key, (sem, val) in need.items():
                            eng.wait_ge(sem, val)
                            seen[key] = val
                            nwait += 1
                        ins = op.fn(eng)
                        if op.dma:
                            ins.then_inc(op.sem, 16)
                        elif op.signal:
                            ins.then_inc(op.sem, 1)
                    if e == "sp":
                        for i in self.final_wait:
                            p = ops[i]
                            if seen.get(id(p.sem), 0) < p.val:
                                eng.wait_ge(p.sem, p.val)
                                seen[id(p.sem)] = p.val
                    self.stats["wait_" + e] = nwait
                return body

            block.tensor(make("pe"))
            block.scalar(make("act"))
            block.vector(make("dve"))
            block.gpsimd(make("pool"))
            block.sync(make("sp"))


def MM(out, lhsT, rhs, start, stop):
    return lambda e: e.matmul(out, lhsT=lhsT, rhs=rhs, start=start, stop=stop)


def ACT(out, in_, func, scale=None, bias=None):
    kw = {}
    if scale is not None:
        kw["scale"] = scale
    if bias is not None:
        kw["bias"] = bias
    return lambda e: e.activation(out=out, in_=in_, func=func, **kw)


def STT(out, in0, scalar, in1, op0, op1):
    return lambda e: e.scalar_tensor_tensor(out=out, in0=in0, scalar=scalar, in1=in1, op0=op0, op1=op1)


def TS(out, in0, s1, op0, s2=None, op1=None):
    if op1 is None:
        return lambda e: e.tensor_scalar(out=out, in0=in0, scalar1=s1, scalar2=None, op0=op0)
    return lambda e: e.tensor_scalar(out=out, in0=in0, scalar1=s1, scalar2=s2, op0=op0, op1=op1)


def TTo(out, in0, in1, op):
    return lambda e: e.tensor_tensor(out=out, in0=in0, in1=in1, op=op)


def CP(out, in_):
    return lambda e: e.tensor_copy(out=out, in_=in_)


def RCP(out, in_):
    return lambda e: e.reciprocal_approx_fast(out=out, in_=in_)


def SCAN(out, d0, d1, init, op0, op1):
    return lambda e: e.tensor_tensor_scan(out=out, data0=d0, data1=d1, initial=init, op0=op0, op1=op1)


def DMA(out, in_):
    return lambda e: e.dma_start(out=out, in_=in_)


def build(NT=8, LAYERS=2, DBG=()):
    SL = NT * TT
    nc = bass.Bass("TRN2", target_bir_lowering=False)
    NBLK = NPRO + NL0 + NL1
    xT = nc.dram_tensor("xT", [1024, SL], F32, kind="ExternalInput").ap()
    memT = nc.dram_tensor("memT", [1024, 256], F32, kind="ExternalInput").ap()
    wst = nc.dram_tensor("wst", [NBLK, 128, 4096], F32, kind="ExternalInput").ap()
    wsm = nc.dram_tensor("wsm", [128, 1024], F32, kind="ExternalInput").ap()
    pv = nc.dram_tensor("pv", [128, NPV], F32, kind="ExternalInput").ap()
    yT = nc.dram_tensor("yT", [1024, SL], F32, kind="ExternalOutput").ap()
    wbf = nc.dram_tensor("wbf", [NBLK, 128, 4096], BF16, kind="Internal").ap()
    kd = nc.dram_tensor("kd", [8, 128, SL], BF16, kind="Internal").ap()
    vd = nc.dram_tensor("vd", [8, 128, SL // 128, 128], BF16, kind="Internal").ap()

    xTv = xT.rearrange("(c p) s -> p c s", p=128)
    yTv = yT.rearrange("(c p) s -> p c s", p=128)
    memTv = memT.rearrange("(c p) s -> p c s", p=128)

    with contextlib.ExitStack() as st:
        def T(name, shape, dt):
            return st.enter_context(nc.sbuf_tensor(name, shape, dt))

        NSLOT = 4
        ring = [T("ring%d" % i, [128, 4096], BF16) for i in range(NSLOT)]
        x = T("x", [128, 8, TT], F32)
        hn = T("hn", [128, 8, TT], BF16)
        ycat = T("ycat", [128, 8, TT], BF16)
        h1 = T("h1", [128, 16, TT], BF16)
        sq = h1
        GA = T("GA", [128, 4, TT], F32)
        ge4 = GA
        u = T("u", [128, 4, TT + 3], F32)
        xq = T("xq", [128, 4, TT], F32)
        pvs = T("pvs", [128, NPV], F32)
        dv = T("dv", [128, 32], F32)
        hstate = T("hstate", [128, 4], F32)
        carry = T("carry", [128, 8], F32)
        ones = T("ones", [128, TT], BF16)
        wg = T("wg", [128, 8, 128], BF16)
        memn = hn
        memf = xq[:].rearrange("p a (b c) -> p (a b) c", c=256)
        XQK = [("xq", m) for m in range(4)]
        kmT = T("kmT", [128, 2, 4, 256], BF16)
        vm = T("vm", [128, 2, 2, 512], BF16)
        NF = 14
        NB = 8
        fsl = [T("fs%d" % i, [128, TT], F32) for i in range(NF)]
        bsl = [T("bs%d" % i, [128, TT], BF16) for i in range(NB)]
        if LAYERS > 1:
            qa = GA[:].bitcast(BF16).rearrange("p a (b c) -> p (a b) c", c=TT)
            ka = T("ka", [128, 8, TT], BF16)
            va = T("va", [128, 8, 4, 128], BF16)
            kst = [T("kst%d" % i, [128, SL], BF16) for i in range(2)]
            vst = [T("vst%d" % i, [128, SL // 128, 128], BF16) for i in range(2)]
        psb = [st.enter_context(nc.psum_tensor("ps%d" % i, [128, TT], F32)) for i in range(8)]

        S = Sched(nc)
        cnt = {"ps": 0, "f": 0, "b": 0}

        cnt["pl"] = 0

        def nextps():
            i = cnt["ps"] % 6
            cnt["ps"] += 1
            return psb[i], ("ps", i)

        def longps():
            i = 6 + cnt["pl"] % 2
            cnt["pl"] += 1
            return psb[i], ("ps", i)

        def tf():
            i = cnt["f"] % NF
            cnt["f"] += 1
            return fsl[i], ("f", i)

        def tb():
            i = cnt["b"] % NB
            cnt["b"] += 1
            return bsl[i], ("b", i)

        cast_done = set()

        def cast(i):
            if i in cast_done:
                return
            cast_done.add(i)
            S.dma(DMA(wbf[i].rearrange("p (a b) -> (p a) b", b=2048),
                      wst[i].rearrange("p (a b) -> (p a) b", b=2048)),
                  w=[("wbf", i)], q="pool")
        order = list(range(NPRO))
        for t in range(NT):
            order += list(range(NPRO, NPRO + NL0))
            if LAYERS > 1:
                order += list(range(NPRO + NL0, NBLK))
        rs = {"issued": 0, "got": 0, "rel": 0}

        def ring_issue():
            while rs["issued"] < len(order) and rs["issued"] - rs["rel"] < NSLOT:
                n = rs["issued"]
                for m_ in range(n, min(n + 8, len(order))):
                    cast(order[m_])
                slot = n % NSLOT
                S.dma(DMA(ring[slot][:], wbf[order[n]]), r=[("wbf", order[n])], w=[("ring", slot)])
                rs["issued"] += 1

        def ring_get(expect):
            n = rs["got"]
            assert order[n] == expect, (n, order[n], expect)
            assert n < rs["issued"]
            rs["got"] += 1
            slot = n % NSLOT
            return ring[slot], ("ring", slot)

        def ring_rel():
            rs["rel"] += 1
            ring_issue()

        S.pool(lambda e: e.memset(ones[:], 1.0), w=["ones"])
        S.pool(lambda e: e.memset(hstate[:], 0.0), w=["hstate"])
        S.pool(lambda e: e.memset(carry[:], 0.0), w=["carry"])
        S.pool(lambda e: e.memset(u[:, :, 0:3], 0.0), w=[("u", j) for j in range(4)])
        if LAYERS > 1:
            S.pool(lambda e: e.memset(va[:, :, :, 64:128], 1.0), w=["va1"])
        S.dma(DMA(pvs[:], pv), w=["pvs"])
        S.dma(DMA(wg[:].rearrange("p a b -> p (a b)"), wsm), w=["wg"], q="pool")
        S.dma(DMA(memf, memTv), w=XQK)
        ring_issue()
        S.act(ACT(dv[:, 0:4], pvs[:, 68:72], AF.Copy, scale=0.5), r=["pvs"], w=["dv"])
        S.act(ACT(dv[:, 4:8], pvs[:, 72:76], AF.Copy, scale=0.5), r=["pvs"], w=["dv"])
        S.act(ACT(dv[:, 28:32], pvs[:, 76:80], AF.Exp, scale=-1.0), r=["pvs"], w=["dv"])
        S.act(ACT(dv[:, 28:32], dv[:, 28:32], AF.Ln, bias=1.0), r=["dv"], w=["dv"])
        S.act(ACT(dv[:, 8:12], dv[:, 28:32], AF.Copy, scale=-8.0), r=["dv"], w=["dv"])
        S.act(ACT(dv[:, 12:16], dv[:, 28:32], AF.Copy, scale=-4.0), r=["dv"], w=["dv"])
        S.act(ACT(dv[:, 16:17], pvs[:, 80:81], AF.Copy, scale=1.0 / math.sqrt(128.0)), r=["pvs"], w=["dv"])
        S.act(ACT(dv[:, 17:18], pvs[:, 82:83], AF.Copy, scale=1.0 / math.sqrt(128.0)), r=["pvs"], w=["dv"])
        S.act(ACT(dv[:, 18:19], pvs[:, 84:85], AF.Copy, scale=0.125), r=["pvs"], w=["dv"])
        S.act(ACT(dv[:, 20:28], pvs[:, 86:94], AF.Copy, scale=-1.0), r=["pvs"], w=["dv"])

        for l in range(2):
            if l >= LAYERS:
                wk, wkk = ring_get(2 * l); ring_rel()
                wv, wvk = ring_get(2 * l + 1); ring_rel()
                continue
            for c in range(8):
                S.act(ACT(sq[:, c, 0:256], memf[:, c, :], AF.Square), r=XQK, w=[("h1", c)])
            pa, pk = nextps()
            for c in range(8):
                S.pe(MM(pa[:, 0:256], ones[:, 0:128], sq[:, c, 0:256], c == 0, c == 7), r=[("h1", c), "ones"], w=[pk])
            sd, sdk = tf()
            S.act(ACT(sd[:, 0:256], pa[:, 0:256], AF.Ln, scale=1.0 / 1024.0, bias=EPS), r=[pk], w=[sdk])
            S.act(ACT(sd[:, 0:256], sd[:, 0:256], AF.Exp, scale=-0.5), r=[sdk], w=[sdk])
            for c in range(8):
                S.dve(STT(memn[:, c, 0:256], memf[:, c, :], pvs[:, 32 + 8 * l + c:33 + 8 * l + c], sd[:, 0:256], MUL, MUL),
                      r=XQK + [sdk, "pvs"], w=[("hn", c)])
            wk, wkk = ring_get(2 * l)
            for m in range(4):
                pa, pk = nextps()
                for c in range(8):
                    S.pe(MM(pa[:, 0:256], wk[:, c * 512 + m * 128:c * 512 + (m + 1) * 128], memn[:, c, 0:256], c == 0, c == 7),
                         r=[wkk, ("hn", c)], w=[pk])
                sqb, sqk = tb()
                S.act(ACT(sqb[:, 0:256], pa[:, 0:256], AF.Square), r=[pk], w=[sqk])
                pb, pbk = nextps()
                S.pe(MM(pb[:, 0:256], ones[:, 0:128], sqb[:, 0:256], True, True), r=[sqk, "ones"], w=[pbk])
                sd, sdk = tf()
                S.act(ACT(sd[:, 0:256], pb[:, 0:256], AF.Ln, scale=1.0 / 128.0, bias=EPS), r=[pbk], w=[sdk])
                S.act(ACT(sd[:, 0:256], sd[:, 0:256], AF.Exp, scale=-0.5), r=[sdk], w=[sdk])
                S.dve(STT(kmT[:, l, m, :], pa[:, 0:256], pvs[:, 81 + 2 * l:82 + 2 * l], sd[:, 0:256], MUL, MUL),
                      r=[pk, sdk, "pvs"], w=[("kmT", l)])
            ring_rel()
            wv, wvk = ring_get(2 * l + 1)
            for mc in range(2):
                pa, pk = nextps()
                for c in range(8):
                    S.pe(MM(pa[:], memn[:, c, mc * 128:(mc + 1) * 128], wv[:, c * 512:(c + 1) * 512], c == 0, c == 7),
                         r=[wvk, ("hn", c)], w=[pk])
                S.act(ACT(vm[:, l, mc, :], pa[:], AF.Copy), r=[pk], w=[("vm", l)])
            ring_rel()

        def rmsnorm(gcol):
            S.phase = 'norm'
            for c in range(8):
                S.act(ACT(sq[:, c, :], x[:, c, :], AF.Square), r=[("x", c)], w=[("h1", c)])
            pa, pk = nextps()
            for c in range(8):
                S.pe(MM(pa[:], ones[:, 0:128], sq[:, c, :], c == 0, c == 7), r=[("h1", c), "ones"], w=[pk])
            sd, sdk = tf()
            S.act(ACT(sd[:], pa[:], AF.Ln, scale=1.0 / 1024.0, bias=EPS), r=[pk], w=[sdk])
            S.act(ACT(sd[:], sd[:], AF.Exp, scale=-0.5), r=[sdk], w=[sdk])
            for c in range(8):
                S.dve(STT(hn[:, c, :], x[:, c, :], pvs[:, gcol + c:gcol + c + 1], sd[:], MUL, MUL),
                      r=[("x", c), sdk, "pvs"], w=[("hn", c)])

        def mem_phase(l):
            S.phase = 'memattn'
            sqs = []
            for m in range(4):
                sqb, sqk = tb()
                S.act(ACT(sqb[:], xq[:, m, :], AF.Square), r=[("xq", m)], w=[sqk])
                sqs.append((sqb, sqk))
            sts = []
            for m in range(4):
                pa, pk = nextps()
                S.pe(MM(pa[:], ones[:, 0:128], sqs[m][0][:], True, True), r=[sqs[m][1], "ones"], w=[pk])
                sts.append((pa, pk))
            sds = []
            for m in range(4):
                sd, sdk = tf()
                S.act(ACT(sd[:], sts[m][0][:], AF.Ln, scale=1.0 / 128.0, bias=EPS), r=[sts[m][1]], w=[sdk])
                S.act(ACT(sd[:], sd[:], AF.Exp, scale=-0.5), r=[sdk], w=[sdk])
                sds.append((sd, sdk))
            qns = []
            for m in range(4):
                qn, qnk = tb()
                S.dve(STT(qn[:], xq[:, m, :], dv[:, 16 + l:17 + l], sds[m][0][:], MUL, MUL),
                      r=[("xq", m), sds[m][1], "dv"], w=[qnk])
                qns.append((qn, qnk))
            for pair in range(2):
                hs_ = (2 * pair, 2 * pair + 1)
                sc = {}
                for m in hs_:
                    for mc in range(2):
                        pa, pk = nextps()
                        S.pe(MM(pa[:], kmT[:, l, m, mc * 128:(mc + 1) * 128], qns[m][0][:], True, True),
                             r=[("kmT", l), qns[m][1]], w=[pk])
                        sc[(m, mc)] = (pa, pk)
                pts = {}
                for m in hs_:
                    for mc in range(2):
                        pt, ptk = tb()
                        S.act(ACT(pt[:], sc[(m, mc)][0][:], AF.Exp), r=[sc[(m, mc)][1]], w=[ptk])
                        pts[(m, mc)] = (pt, ptk)
                acc = {}
                for m in hs_:
                    po, pok = nextps()
                    pd, pdk = nextps()
                    for mc in range(2):
                        S.pe(MM(po[:], vm[:, l, mc, m * 128:(m + 1) * 128], pts[(m, mc)][0][:], mc == 0, mc == 1),
                             r=[("vm", l), pts[(m, mc)][1]], w=[pok])
                    for mc in range(2):
                        S.pe(MM(pd[:], ones[:, 0:128], pts[(m, mc)][0][:], mc == 0, mc == 1),
                             r=["ones", pts[(m, mc)][1]], w=[pdk])
                    acc[m] = (po, pok, pd, pdk)
                rds = {}
                for m in hs_:
                    rd, rdk = tf()
                    S.act(ACT(rd[:], acc[m][2][:], AF.Ln), r=[acc[m][3]], w=[rdk])
                    S.act(ACT(rd[:], rd[:], AF.Exp, scale=-1.0), r=[rdk], w=[rdk])
                    rds[m] = (rd, rdk)
                for m in hs_:
                    S.dve(TTo(ycat[:, 4 + m, :], acc[m][0][:], rds[m][0][:], MUL), r=[acc[m][1], rds[m][1]],
                          w=[("ycat", 4 + m)])

        def bg_step(n=1):
            pass

        def xq_proj(blk):
            wx, wxk = ring_get(blk)
            for m in range(4):
                pa, pk = nextps()
                for c in range(8):
                    S.pe(MM(pa[:], wx[:, c * 512 + m * 128:c * 512 + (m + 1) * 128], hn[:, c, :], c == 0, c == 7),
                         r=[wxk, ("hn", c)], w=[pk])
                S.act(ACT(xq[:, m, :], pa[:], AF.Copy), r=[pk], w=[("xq", m)])
            ring_rel()

        def out_proj_mlp(l, base):
            S.phase = 'outproj'
            for half in range(2):
                wo, wok = ring_get(base + half)
                for ff in range(4):
                    f = half * 4 + ff
                    pa, pk = nextps()
                    for c in range(8):
                        S.pe(MM(pa[:], wo[:, c * 512 + ff * 128:c * 512 + (ff + 1) * 128], ycat[:, c, :], c == 0, c == 7),
                             r=[wok, ("ycat", c)], w=[pk])
                    S.dve(TTo(x[:, f, :], x[:, f, :], pa[:], ADD), r=[pk, ("x", f)], w=[("x", f)])
                ring_rel()
            rmsnorm(8 + 16 * l)
            S.phase = 'mlp'
            for hh in range(2):
                for g in range(4):
                    w1, w1k = ring_get(base + 2 + hh * 8 + g)
                    for j in range(4):
                        pa, pk = nextps()
                        for c in range(8):
                            S.pe(MM(pa[:], w1[:, c * 512 + j * 128:c * 512 + (j + 1) * 128], hn[:, c, :], c == 0, c == 7),
                                 r=[w1k, ("hn", c)], w=[pk])
                        rl, rlk = tf()
                        S.act(ACT(rl[:], pa[:], AF.Relu), r=[pk], w=[rlk])
                        S.pool(TTo(h1[:, g * 4 + j, :], rl[:], rl[:], MUL), r=[rlk], w=[("h1", g * 4 + j)])
                    ring_rel()
                for fp in range(4):
                    w2, w2k = ring_get(base + 2 + hh * 8 + 4 + fp)
                    for ff in range(2):
                        f = fp * 2 + ff
                        pa, pk = nextps()
                        for k in range(16):
                            S.pe(MM(pa[:], w2[:, k * 256 + ff * 128:k * 256 + (ff + 1) * 128], h1[:, k, :], k == 0, k == 15),
                                 r=[w2k, ("h1", k)], w=[pk])
                        S.dve(TTo(x[:, f, :], x[:, f, :], pa[:], ADD), r=[pk, ("x", f)], w=[("x", f)])
                    ring_rel()

        C0 = math.sqrt(2.0 / math.pi)

        def layer0(t):
            base = NPRO
            rmsnorm(0)
            S.phase = 'l0.inproj'
            xq_proj(base + 0)
            wu, wuk = ring_get(base + 1)
            for j in range(4):
                pa, pk = nextps()
                for c in range(8):
                    S.pe(MM(pa[:], wu[:, c * 512 + j * 128:c * 512 + (j + 1) * 128], hn[:, c, :], c == 0, c == 7),
                         r=[wuk, ("hn", c)], w=[pk])
                S.act(ACT(u[:, j, 3:TT + 3], pa[:], AF.Copy), r=[pk], w=[("u", j)])
                bg_step(2)
            ring_rel()
            wgt, wgk = ring_get(base + 2)
            for j in range(4):
                pa, pk = nextps()
                for c in range(8):
                    S.pe(MM(pa[:], wgt[:, c * 512 + j * 128:c * 512 + (j + 1) * 128], hn[:, c, :], c == 0, c == 7),
                         r=[wgk, ("hn", c)], w=[pk])
                gs, gsk = tf()
                g2, g2k = tf()
                gek = [("qa", 2 * j), ("qa", 2 * j + 1)]
                S.act(ACT(gs[:], pa[:], AF.Copy), r=[pk], w=[gsk])
                S.act(ACT(g2[:], pa[:], AF.Square), r=[pk], w=[g2k])
                S.dve(TS(g2[:], g2[:], 0.044715, MUL, 1.0, ADD), r=[g2k], w=[g2k])
                S.dve(TTo(g2[:], g2[:], gs[:], MUL), r=[g2k, gsk], w=[g2k])
                S.act(ACT(g2[:], g2[:], AF.Tanh, scale=C0), r=[g2k], w=[g2k])
                S.dve(STT(ge4[:, j, :], g2[:], 1.0, gs[:], ADD, MUL), r=[g2k, gsk], w=gek)
                bg_step(2)
            ring_rel()
            mem_phase(0)
            S.phase = 'l0.rec'
            for j in range(4):
                gek = [("qa", 2 * j), ("qa", 2 * j + 1)]
                uc, uck = tf()
                S.dve(TS(uc[:], u[:, j, 3:TT + 3], pvs[:, 60 + j:61 + j], MUL, pvs[:, 64 + j:65 + j], ADD),
                      r=[("u", j), "pvs"], w=[uck])
                for tap in (2, 1, 0):
                    S.dve(STT(uc[:], u[:, j, tap:tap + TT], pvs[:, 48 + 4 * tap + j:49 + 4 * tap + j], uc[:], MUL, ADD),
                          r=[("u", j), "pvs", uck], w=[uck])
                S.pool(CP(u[:, j, 0:3], u[:, j, TT:TT + 3]), r=[("u", j)], w=[("u", j)])
                ucb, ucbk = tb()
                S.act(ACT(ucb[:], uc[:], AF.Copy), r=[uck], w=[ucbk])
                pr, prk = nextps()
                S.pe(MM(pr[:], wg[:, j, :], ucb[:], True, True), r=["wg", ucbk], w=[prk])
                pi, pik = nextps()
                S.pe(MM(pi[:], wg[:, 4 + j, :], ucb[:], True, True), r=["wg", ucbk], w=[pik])
                tr, trk = tf()
                S.act(ACT(tr[:], pr[:], AF.Tanh, scale=0.5, bias=dv[:, j:j + 1]), r=[prk, "dv"], w=[trk])
                av, avk = tf()
                S.act(ACT(av[:], tr[:], AF.Exp, scale=dv[:, 12 + j:13 + j], bias=dv[:, 12 + j:13 + j]), r=[trk, "dv"], w=[avk])
                a2, a2k = tf()
                S.act(ACT(a2[:], tr[:], AF.Exp, scale=dv[:, 8 + j:9 + j], bias=dv[:, 8 + j:9 + j]), r=[trk, "dv"], w=[a2k])
                ti, tik = tf()
                S.act(ACT(ti[:], pi[:], AF.Tanh, scale=0.5, bias=dv[:, 4 + j:5 + j]), r=[pik, "dv"], w=[tik])
                S.dve(STT(ti[:], ti[:], 1.0, uc[:], ADD, MUL), r=[tik, uck], w=[tik])
                S.act(ACT(a2[:], a2[:], AF.Ln, scale=-1.0, bias=1.0), r=[a2k], w=[a2k])
                S.act(ACT(a2[:], a2[:], AF.Exp, scale=0.5), r=[a2k], w=[a2k])
                S.dve(STT(ti[:], ti[:], 0.5, a2[:], MUL, MUL), r=[tik, a2k], w=[tik])
                hs, hsk = tf()
                S.dve(SCAN(hs[:], av[:], ti[:], hstate[:, j:j + 1], MUL, ADD), r=[avk, tik, "hstate"], w=[hsk])
                S.dve(CP(hstate[:, j:j + 1], hs[:, TT - 1:TT]), r=[hsk], w=["hstate"])
                S.dve(STT(ycat[:, j, :], hs[:], 0.5, ge4[:, j, :], MUL, MUL), r=[hsk] + gek, w=[("ycat", j)])
            out_proj_mlp(0, base + 3)

        def layer1(t):
            base = NPRO + NL0
            rmsnorm(16)
            S.phase = 'l1.qk'
            xq_proj(base + 0)
            S.phase = 'l1.qk'
            R = slice(64, 70)

            def q_tail(h, pa, pk, sqb, sqk):
                pb, pbk = nextps()
                S.pe(MM(pb[0:64, :], ones[0:64, 0:64], sqb[0:64, :], True, True), r=[sqk, "ones"], w=[pbk])
                sd, sdk = tf()
                S.act(ACT(sd[0:64, :], pb[0:64, :], AF.Ln, scale=1.0 / 64.0, bias=EPS), r=[pbk], w=[sdk])
                S.act(ACT(sd[0:64, :], sd[0:64, :], AF.Exp, scale=-0.5), r=[sdk], w=[sdk])
                S.dve(STT(qa[0:64, h, :], pa[0:64, :], dv[0:64, 18:19], sd[0:64, :], MUL, MUL),
                      r=[pk, sdk, "dv"], w=[("qa", h)])
                e1, e1k = tf()
                S.act(ACT(e1[R, :], pa[R, :], AF.Exp, scale=-1.0, bias=dv[R, 20 + h:21 + h]), r=[pk, "dv"], w=[e1k])
                S.act(ACT(e1[R, :], e1[R, :], AF.Ln, bias=1.0), r=[e1k], w=[e1k])
                cs, csk = tf()
                S.dve(SCAN(cs[R, :], ones[R, :], e1[R, :], carry[R, h:h + 1], MUL, SUB), r=[e1k, "ones", "carry"], w=[csk])
                S.pool(CP(carry[R, h:h + 1], cs[R, TT - 1:TT]), r=[csk], w=["carry"])
                t1, t1k = tb()
                S.dve(CP(t1[R, :], cs[R, :]), r=[csk], w=[t1k])
                t2, t2k = tf()
                S.dve(STT(t2[R, :], t1[R, :], pvs[R, 94:95], cs[R, :], MUL, ADD), r=[t1k, csk, "pvs"], w=[t2k])
                t3, t3k = tb()
                S.dve(CP(t3[R, :], t2[R, :]), r=[t2k], w=[t3k])
                S.dve(STT(t2[R, :], t3[R, :], pvs[R, 95:96], t2[R, :], MUL, ADD), r=[t3k, t2k, "pvs"], w=[t2k])
                S.pool(TS(qa[R, h, :], t2[R, :], pvs[R, 96:97], MUL, pvs[R, 97:98], ADD), r=[t2k, "pvs"], w=[("qa", h)])
                S.pool(TS(ka[R, h, :], t2[R, :], pvs[R, 98:99], MUL, pvs[R, 99:100], ADD), r=[t2k, "pvs"], w=[("ka", h)])

            def k_tail(h, pa, pk, sqb, sqk):
                pb, pbk = nextps()
                S.pe(MM(pb[0:64, :], ones[0:64, 0:64], sqb[0:64, :], True, True), r=[sqk, "ones"], w=[pbk])
                sd, sdk = tf()
                S.act(ACT(sd[0:64, :], pb[0:64, :], AF.Ln, scale=1.0 / 64.0, bias=EPS), r=[pbk], w=[sdk])
                S.act(ACT(sd[0:64, :], sd[0:64, :], AF.Exp, scale=-0.5), r=[sdk], w=[sdk])
                S.dve(STT(ka[0:64, h, :], pa[0:64, :], pvs[0:64, 85:86], sd[0:64, :], MUL, MUL),
                      r=[pk, sdk, "pvs"], w=[("ka", h)])

            pend = None
            for blk in range(4):
                ww, wwk = ring_get(base + 1 + blk)
                isq = blk < 2
                M = 70 if isq else 64
                for hh in range(4):
                    h = (blk % 2) * 4 + hh
                    pa, pk = nextps()
                    for c in range(8):
                        S.pe(MM(pa[0:M, :], ww[:, c * 512 + hh * 128:c * 512 + hh * 128 + M], hn[:, c, :], c == 0, c == 7),
                             r=[wwk, ("hn", c)], w=[pk])
                    sqb, sqk = tb()
                    S.act(ACT(sqb[0:64, :], pa[0:64, :], AF.Square), r=[pk], w=[sqk])
                    if pend is not None:
                        pend[0](*pend[1])
                    pend = (q_tail if isq else k_tail, (h, pa, pk, sqb, sqk))
                    bg_step(1)
                ring_rel()
            pend[0](*pend[1])
            S.phase = 'l1.v_xq'
            wv, wvk = ring_get(base + 5)
            for tc in range(4):
                pa, pk = nextps()
                for c in range(8):
                    S.pe(MM(pa[:], hn[:, c, tc * 128:(tc + 1) * 128], wv[:, c * 512:(c + 1) * 512], c == 0, c == 7),
                         r=[wvk, ("hn", c)], w=[pk])
                S.act(ACT(va[:, :, tc, 0:64], pa[:].rearrange("p (h d) -> p h d", d=64), AF.Copy), r=[pk], w=[("va", tc)])
                bg_step(1)
            ring_rel()
            if t + 1 < NT and 'nostore' not in DBG:
                for h in range(8):
                    S.dma(DMA(kd[h, :, t * TT:(t + 1) * TT], ka[:, h, :]), r=[("ka", h)], w=[("kd", t, h)])
                    S.dma(DMA(vd[h, :, t * 4:(t + 1) * 4, :], va[:, h, :, :]),
                          r=[("va", tc) for tc in range(4)] + ["va1"], w=[("vd", t, h)])
            mem_phase(1)
            S.phase = 'l1.attn'
            for h in range(8):
                so = h % 2
                if t > 0 and 'noload' not in DBG:
                    S.dma(DMA(kst[so][:, 0:t * TT], kd[h, :, 0:t * TT]), r=[("kd", tt, h) for tt in range(t)], w=[("kst", so)])
                    S.dma(DMA(vst[so][:, 0:t * 4, :], vd[h, :, 0:t * 4, :]), r=[("vd", tt, h) for tt in range(t)], w=[("vst", so)])
                po, pok = longps()
                nkc = 4 * (t + 1)
                seq = [4 * t + d for d in range(4)] + list(range(4 * t))
                if 'noold' in DBG:
                    seq = seq[:4]; nkc = 4
                LOOK = 3
                qk = []

                def issue_qk(kc):
                    if kc >= 4 * t:
                        d = kc - 4 * t
                        qlo = d * 128
                        kl = ka[0:70, h, d * 128:(d + 1) * 128]
                        vl = va[:, h, d, :]
                        kr = [("ka", h)]
                        vr = [("va", d), "va1"]
                    else:
                        d = -1
                        qlo = 0
                        kl = kst[so][0:70, kc * 128:(kc + 1) * 128]
                        vl = vst[so][:, kc, :]
                        kr = [("kst", so)]
                        vr = [("vst", so)]
                    ps_, psk = nextps()
                    S.pe(MM(ps_[:, qlo:TT], kl, qa[0:70, h, qlo:TT], True, True), r=kr + [("qa", h)], w=[psk])
                    qk.append((ps_, psk, qlo, d, vl, vr))

                for n in range(len(seq)):
                    while len(qk) < min(n + 1 + LOOK, len(seq)):
                        issue_qk(seq[len(qk)])
                    ps_, psk, qlo, d, vl, vr = qk[n]
                    pt, ptk = tb()
                    S.act(ACT(pt[:, qlo:TT], ps_[:, qlo:TT], AF.Exp), r=[psk], w=[ptk])
                    if d >= 0:
                        S.pool(lambda e, pt=pt, qlo=qlo: e.affine_select(
                            out=pt[:, qlo:qlo + 128], in_=pt[:, qlo:qlo + 128], pattern=[[1, 128]],
                            compare_op=ALU.is_ge, fill=0.0, base=0, channel_multiplier=-1), r=[ptk], w=[ptk])
                    S.pe(MM(po[:, qlo:TT], vl, pt[:, qlo:TT], n == 0, n == nkc - 1), r=vr + [ptk], w=[pok])
                    if n % 2 == 1:
                        bg_step(1)
                rd, rdk = tf()
                S.act(ACT(rd[0:64, :], po[64:128, :], AF.Ln), r=[pok], w=[rdk])
                S.act(ACT(rd[0:64, :], rd[0:64, :], AF.Exp, scale=-1.0), r=[rdk], w=[rdk])
                po_ = (h % 2) * 64
                S.dve(TTo(ycat[po_:po_ + 64, h // 2, :], po[0:64, :], rd[0:64, :], MUL), r=[pok, rdk], w=[("ycat", h // 2)])
            out_proj_mlp(1, base + 6)

        for t in range(NT):
            S.dma(DMA(x[:], xTv[:, :, t * TT:(t + 1) * TT]), w=[("x", c) for c in range(8)])
            layer0(t)
            if LAYERS > 1:
                layer1(t)
            S.dma(DMA(yTv[:, :, t * TT:(t + 1) * TT], x[:]), r=[("x", c) for c in range(8)], final=True)
        assert rs["got"] == len(order), (rs, len(order))
        S.emit()
        build.stats = S.stats
        build.tags = S.tags
    return nc


def _kblock(w):
    return np.ascontiguousarray(w.reshape(8, 128, 512).transpose(1, 0, 2)).reshape(128, 4096)


def _pcol(v, n):
    return np.ascontiguousarray(np.asarray(v).reshape(n, 128).T)


def prepare(inp, LAYERS=2):
    f32 = np.float32
    blocks = []
    for l in range(2):
        blocks.append(_kblock(inp["w_mem_kv"][l][:, 0:512]))
        blocks.append(_kblock(inp["w_mem_kv"][l][:, 512:1024]))

    def mlp_blocks(l):
        out = []
        w1 = inp["w_mlp_in"][l]; w2 = inp["w_mlp_out"][l]
        for hh in range(2):
            for g in range(4):
                out.append(_kblock(w1[:, (hh * 4 + g) * 512:(hh * 4 + g + 1) * 512]))
            for fp in range(4):
                blk = w2[hh * 2048:(hh + 1) * 2048, fp * 256:(fp + 1) * 256]
                out.append(np.ascontiguousarray(blk.reshape(16, 128, 256).transpose(1, 0, 2)).reshape(128, 4096))
        return out

    wa = inp["w_in_a"][0]
    for part in (2, 0, 1):
        blocks.append(_kblock(wa[:, part * 512:(part + 1) * 512]))
    for half in range(2):
        blocks.append(_kblock(inp["w_out"][0][:, half * 512:(half + 1) * 512]))
    blocks += mlp_blocks(0)
    wb = inp["w_in_b"][0]
    blocks.append(_kblock(wb[:, 1544:2056]))
    for jb in range(2):
        blk = np.zeros((1024, 4, 128), f32)
        for hh in range(4):
            h = jb * 4 + hh
            blk[:, hh, 0:64] = wb[:, h * 64:(h + 1) * 64]
            blk[:, hh, 64:70] = wb[:, 1536 + h:1537 + h]
        blocks.append(_kblock(blk.reshape(1024, 512)))
    for jb in range(2):
        blk = np.zeros((1024, 4, 128), f32)
        for hh in range(4):
            h = jb * 4 + hh
            blk[:, hh, 0:64] = wb[:, 512 + h * 64:512 + (h + 1) * 64]
        blocks.append(_kblock(blk.reshape(1024, 512)))
    blocks.append(_kblock(wb[:, 1024:1536]))
    for half in range(2):
        blocks.append(_kblock(inp["w_out"][1][:, half * 512:(half + 1) * 512]))
    blocks += mlp_blocks(1)
    wst = np.stack(blocks).astype(f32)
    assert wst.shape[0] == NPRO + NL0 + NL1, wst.shape

    wsm = np.zeros((128, 8, 128), f32)
    for gi, wgate in enumerate((inp["w_rgate"][0], inp["w_igate"][0])):
        for j in range(4):
            for b in range(2):
                wsm[b * 64:(b + 1) * 64, gi * 4 + j, b * 64:(b + 1) * 64] = wgate[2 * j + b]
    wsm = wsm.reshape(128, 1024)

    pvv = np.zeros((128, NPV), f32)
    pvv[:, 0:8] = _pcol(inp["norm_mix"][0], 8)
    pvv[:, 8:16] = _pcol(inp["norm_mlp"][0], 8)
    pvv[:, 16:24] = _pcol(inp["norm_mix"][1], 8)
    pvv[:, 24:32] = _pcol(inp["norm_mlp"][1], 8)
    pvv[:, 32:40] = _pcol(inp["norm_mem"][0], 8)
    pvv[:, 40:48] = _pcol(inp["norm_mem"][1], 8)
    for tap in range(4):
        pvv[:, 48 + 4 * tap:52 + 4 * tap] = _pcol(inp["conv_w"][0][tap], 4)
    pvv[:, 64:68] = _pcol(inp["conv_b"][0], 4)
    pvv[:, 68:72] = _pcol(inp["b_rgate"][0], 4)
    pvv[:, 72:76] = _pcol(inp["b_igate"][0], 4)
    pvv[:, 76:80] = _pcol(inp["lru_lambda"][0], 4)
    pvv[:, 80] = inp["xa_q_gain"][0]; pvv[:, 81] = inp["xa_k_gain"][0]
    pvv[:, 82] = inp["xa_q_gain"][1]; pvv[:, 83] = inp["xa_k_gain"][1]
    pvv[0:64, 84] = inp["fox_q_gain"][0]; pvv[64:128, 84] = inp["fox_q_gain"][0]
    pvv[0:64, 85] = inp["fox_k_gain"][0]; pvv[64:128, 85] = inp["fox_k_gain"][0]
    pvv[:, 86:94] = np.asarray(inp["b_forget"][0])[None, :]
    pvv[[65, 66, 68, 69], 94] = -1.0
    pvv[[66, 69], 95] = -1.0
    pvv[64:67, 96] = 1.0; pvv[67:70, 97] = 1.0
    pvv[67:70, 98] = -1.0; pvv[64:67, 99] = 1.0
    return wst, wsm, pvv


_NC_CACHE = {}


def kernel(**inputs):
    inp = {k: np.asarray(v) for k, v in inputs.items()}
    B, SEQ, D = inp["x"].shape
    NT = SEQ // TT
    wst, wsm, pvv = prepare(inp)
    key = (NT, 2)
    if key not in _NC_CACHE:
        _NC_CACHE[key] = build(NT, 2)
    nc = _NC_CACHE[key]
    in_maps = []
    for b in range(B):
        in_maps.append({
            "xT": np.ascontiguousarray(inp["x"][b].T.astype(np.float32)),
            "memT": np.ascontiguousarray(inp["mem"][b].T.astype(np.float32)),
            "wst": wst, "wsm": wsm, "pv": pvv,
        })
    res = run_bass_kernel_spmd(nc, in_maps, core_ids=list(range(B)))
    out = np.stack([np.ascontiguousarray(r["yT"].T) for r in res.results]).astype(np.float32)
    return out
```
